# Optimizing a Trainium2 kernel written in Bass

```python
import jax, jax.numpy as jnp
from jax import lax
import numpy as np

D_MODEL = 1024
BATCH = 8
SEQ = 2048
DEPTH = 1

D_MIX = 2 * D_MODEL
D_A = D_MIX // 2
D_B = D_MIX - D_A
A_GROUPS = 8
A_GROUP_DIM = D_A // A_GROUPS
A_CHUNK = 128
B_HEADS = 4
B_DK = D_B // 2 // B_HEADS
B_DV = D_B // B_HEADS
B_GATE_RANK = 16
B_GATE_TAU = 16.0
B_CHUNK = 64
D_IN = 3 * D_A + 2 * B_HEADS * B_DK + 2 * D_B + B_GATE_RANK
EPS = 1e-6

kernel_name = "hybrid_gmlp_gla_parallel_heads"


def rmsnorm(x, g):
    xf = x.astype(jnp.float32)
    y = xf * lax.rsqrt(jnp.mean(xf * xf, axis=-1, keepdims=True) + EPS)
    return (y * g.astype(jnp.float32)).astype(x.dtype)


def layernorm(x, g, b):
    xf = x.astype(jnp.float32)
    mu = jnp.mean(xf, axis=-1, keepdims=True)
    xc = xf - mu
    y = xc * lax.rsqrt(jnp.mean(xc * xc, axis=-1, keepdims=True) + EPS)
    return (y * g.astype(jnp.float32) + b.astype(jnp.float32)).astype(x.dtype)


def chunked_sgu(u, v, ln_g, ln_b, w_s, b_s):
    bsz, s = u.shape[:2]
    n = s // A_CHUNK
    vn = layernorm(v, ln_g, ln_b).reshape(bsz, n, A_CHUNK, A_GROUPS, A_GROUP_DIM)
    causal = jnp.tril(jnp.ones((A_CHUNK, A_CHUNK), dtype=bool))
    w = jnp.where(causal[None], w_s, jnp.zeros_like(w_s))
    sp = jnp.einsum('gts,bnsgc->bntgc', w, vn) + b_s.T[None, None, :, :, None]
    return u * sp.reshape(bsz, s, D_A).astype(u.dtype)


def gla_chunked(q, k, v, log_a):
    bsz, s = q.shape[:2]
    n = s // B_CHUNK
    f32 = jnp.float32

    def blk(t, d):
        return t.astype(f32).reshape(bsz, n, B_CHUNK, B_HEADS, d).transpose(0, 3, 1, 2, 4)

    q = blk(q, B_DK) * (B_DK ** -0.5)
    k = blk(k, B_DK)
    v = blk(v, B_DV)
    b = jnp.cumsum(blk(log_a, B_DK), axis=3)
    b_last = b[:, :, :, -1:, :]
    b_ref = b[:, :, :, B_CHUNK // 2 - 1:B_CHUNK // 2, :]

    qe = q * jnp.exp(b - b_ref)
    ke = k * jnp.exp(b_ref - b)
    scores = jnp.einsum('bhncd,bhnsd->bhncs', qe, ke)
    causal = jnp.tril(jnp.ones((B_CHUNK, B_CHUNK), dtype=bool))
    scores = jnp.where(causal, scores, 0.0)
    o_intra = jnp.einsum('bhncs,bhnsv->bhncv', scores, v)

    chunk_kv = jnp.einsum('bhncd,bhncv->bhndv', k * jnp.exp(b_last - b), v)
    decay = jnp.exp(b_last[:, :, :, 0, :])

    def step(state, inp):
        dec, kv = inp
        return dec[..., None] * state + kv, state

    s0 = jnp.zeros((bsz, B_HEADS, B_DK, B_DV), f32)
    _, s_prev = lax.scan(step, s0, (decay.transpose(2, 0, 1, 3), chunk_kv.transpose(2, 0, 1, 3, 4)))
    s_prev = s_prev.transpose(1, 2, 0, 3, 4)
    o_inter = jnp.einsum('bhncd,bhndv->bhncv', q * jnp.exp(b), s_prev)

    o = o_intra + o_inter
    return o.transpose(0, 2, 3, 1, 4).reshape(bsz, s, B_HEADS, B_DV)


def hybrid_layer(x, pre_g, w_in, w_a2, b_a2, a_ln_g, a_ln_b, a_w_s, a_b_s, b_norm_g, w_out, post_g):
    bsz, s, _ = x.shape
    h = rmsnorm(x, pre_g)
    z = jnp.einsum('bsd,de->bse', h, w_in)
    sizes = [D_A, D_A, D_A, B_HEADS * B_DK, B_HEADS * B_DK, D_B, D_B, B_GATE_RANK]
    cuts = np.cumsum(sizes)[:-1].tolist()
    a_u, a_v, a_gate, b_q, b_k, b_v, b_gate, b_lr = jnp.split(z, cuts, axis=-1)

    a_out = chunked_sgu(jax.nn.gelu(a_u, approximate=False), jax.nn.gelu(a_v, approximate=False),
                        a_ln_g, a_ln_b, a_w_s, a_b_s)
    a_out = a_out * jax.nn.silu(a_gate)

    gate_logit = jnp.einsum('bsr,re->bse', b_lr.astype(jnp.float32), w_a2.astype(jnp.float32)) + b_a2.astype(jnp.float32)
    log_a = jax.nn.log_sigmoid(gate_logit) / B_GATE_TAU
    o = gla_chunked(b_q.reshape(bsz, s, B_HEADS, B_DK), b_k.reshape(bsz, s, B_HEADS, B_DK),
                    b_v.reshape(bsz, s, B_HEADS, B_DV), log_a.reshape(bsz, s, B_HEADS, B_DK))
    o = rmsnorm(o, b_norm_g.reshape(B_HEADS, B_DV)).reshape(bsz, s, D_B).astype(x.dtype)
    b_out = o * jax.nn.silu(b_gate)

    mixed = jnp.concatenate([a_out, b_out], axis=-1)
    y = jnp.einsum('bse,ed->bsd', mixed, w_out)
    return x + rmsnorm(y, post_g)


def setup_inputs(seed: int = 0) -> dict:
    key = jax.random.key(seed)
    ks = jax.random.split(key, 13)
    f32 = jnp.float32
    nrm = lambda k, shp, sc: jax.random.normal(k, shp, f32) * sc
    return {
        "x": jax.random.normal(ks[0], (BATCH, SEQ, D_MODEL), f32),
        "pre_norm_g": 1.0 + nrm(ks[1], (DEPTH, D_MODEL), 0.02),
        "w_in": nrm(ks[2], (DEPTH, D_MODEL, D_IN), D_MODEL ** -0.5),
        "w_a2": nrm(ks[3], (DEPTH, B_GATE_RANK, B_HEADS * B_DK), B_GATE_RANK ** -0.5),
        "b_a2": nrm(ks[4], (DEPTH, B_HEADS * B_DK), 0.1),
        "a_ln_g": 1.0 + nrm(ks[5], (DEPTH, D_A), 0.02),
        "a_ln_b": nrm(ks[6], (DEPTH, D_A), 0.02),
        "a_w_s": nrm(ks[7], (DEPTH, A_GROUPS, A_CHUNK, A_CHUNK), A_CHUNK ** -0.5),
        "a_b_s": 1.0 + nrm(ks[8], (DEPTH, A_GROUPS, A_CHUNK), 0.02),
        "b_norm_g": 1.0 + nrm(ks[9], (DEPTH, D_B), 0.02),
        "w_out": nrm(ks[10], (DEPTH, D_MIX, D_MODEL), D_MIX ** -0.5),
        "post_norm_g": 1.0 + nrm(ks[11], (DEPTH, D_MODEL), 0.02),
    }


def reference(x, pre_norm_g, w_in, w_a2, b_a2, a_ln_g, a_ln_b, a_w_s, a_b_s, b_norm_g, w_out, post_norm_g):
    for l in range(DEPTH):
        x = hybrid_layer(x, pre_norm_g[l], w_in[l], w_a2[l], b_a2[l], a_ln_g[l], a_ln_b[l],
                         a_w_s[l], a_b_s[l], b_norm_g[l], w_out[l], post_norm_g[l])
    return x
```

```python
import math
import numpy as np
import concourse.bass as bass
import concourse.mybir as mybir
from concourse.bass_utils import run_bass_kernel_spmd

F32 = mybir.dt.float32
BF16 = mybir.dt.bfloat16
AF = mybir.ActivationFunctionType
ALU = mybir.AluOpType

D = 1024
SEQ = 2048
NT = SEQ // 128
D_IN = 6160
EPS = 1e-6
C_AU, C_AV, C_AG, C_Q, C_K, C_V, C_BG, C_LR = 0, 1024, 2048, 3072, 3584, 4096, 5120, 6144
WBLOCKS = [("lqk", C_Q, 1040), ("v0", C_V, 512), ("v1", C_V + 512, 512),
           ("av0", C_AV, 512), ("av1", C_AV + 512, 512), ("au0", C_AU, 512), ("au1", C_AU + 512, 512),
           ("bg0", C_BG, 512), ("bg1", C_BG + 512, 512), ("ag0", C_AG, 512), ("ag1", C_AG + 512, 512)]
K_ID, K_MASK, K_TRIL, K_END = 0, 128, 640, 768


class Sched:
    def __init__(self, nc):
        self.nc = nc
        self.eng = {"pe": nc.tensor, "act": nc.scalar, "dve": nc.vector,
                    "pool": nc.gpsimd, "sp": nc.sync}
        self.ops = []
        self.writers = {}
        self.readers = {}

    def _reduce(self, idxs):
        best = {}
        out = []
        for i in idxs:
            o = self.ops[i]
            if o["dma"]:
                out.append(i)
            else:
                e = o["eng"]
                if e not in best or best[e] < i:
                    best[e] = i
        return sorted(set(out) | set(best.values()))

    def op(self, eng, fn, reads=(), writes=(), partial=(), dma=False, semkey=None):
        idx = len(self.ops)
        raw = set()
        other = set()
        for b in reads:
            raw.update(self.writers.get(b, ()))
        for b in list(writes) + list(partial):
            other.update(self.writers.get(b, ()))
            other.update(self.readers.get(b, ()))
        self.ops.append(dict(eng=eng, fn=fn, raw=raw, other=other - raw, dma=dma, semkey=semkey))
        for b in reads:
            self.readers[b] = self._reduce(list(self.readers.get(b, [])) + [idx])
        for b in writes:
            self.writers[b] = [idx]
            self.readers[b] = []
        for b in partial:
            self.writers[b] = self._reduce(list(self.writers.get(b, [])) + [idx])
        return idx

    def dma(self, queue, out, in_, reads=(), writes=(), partial=(), semkey=None, **kw):
        eng = self.eng[queue]
        return self.op(queue, lambda: eng.dma_start(out=out, in_=in_, **kw), reads=reads,
                       writes=writes, partial=partial, dma=True, semkey=semkey)

    def emit(self, final_wait_eng="sp"):
        nc = self.nc
        ops = self.ops
        need = []
        signal = [False] * len(ops)
        for i, o in enumerate(ops):
            e = o["eng"]
            deps = set()
            for d in o["raw"]:
                od = ops[d]
                if od["dma"] or od["eng"] != e or e != "pe":
                    deps.add(d)
            for d in o["other"]:
                od = ops[d]
                if od["dma"] or od["eng"] != e:
                    deps.add(d)
            deps = self._reduce(deps)
            need.append(deps)
            for d in deps:
                signal[d] = True
        for i, o in enumerate(ops):
            if o["dma"]:
                signal[i] = True
        sems = {}

        def getsem(name):
            if name not in sems:
                sems[name] = nc.alloc_semaphore("s_" + "_".join(str(x) for x in name))
            return sems[name]

        cnt = {}
        sig = {}
        for i, o in enumerate(ops):
            if not signal[i]:
                continue
            if o["dma"]:
                key = ("dma", o["semkey"] if o["semkey"] is not None else i)
                cnt[key] = cnt.get(key, 0) + 16
            else:
                key = ("eng", o["eng"])
                cnt[key] = cnt.get(key, 0) + 1
            sig[i] = (key, cnt[key])
        seen = {e: {} for e in self.eng}
        nwaits = 0
        for i, o in enumerate(ops):
            e = o["eng"]
            engine = self.eng[e]
            for d in need[i]:
                key, val = sig[d]
                if seen[e].get(key, 0) >= val:
                    continue
                seen[e][key] = val
                engine.wait_ge(getsem(key), val)
                nwaits += 1
            ins = o["fn"]()
            if signal[i]:
                key, val = sig[i]
                ins.then_inc(getsem(key), 16 if o["dma"] else 1)
        engine = self.eng[final_wait_eng]
        for key, val in cnt.items():
            if key[0] == "dma":
                if seen[final_wait_eng].get(key, 0) >= val:
                    continue
                engine.wait_ge(getsem(key), val)
        self.stats = dict(n_ops=len(ops), n_waits=nwaits, n_sems=len(sems),
                          per_eng={e: sum(1 for o in ops if o["eng"] == e) for e in self.eng})
        return self.stats


def host_consts():
    c = np.zeros((128, K_END), np.float32)
    c[:, K_ID:K_ID + 128] = np.eye(128, dtype=np.float32)
    j = np.arange(128)[:, None]
    i = np.arange(128)[None, :]
    same = (j // 64) == (i // 64)
    mask = (same & (j <= i)).astype(np.float32)
    c[:, K_MASK:K_MASK + 512] = np.tile(mask, (1, 4))
    c[:, K_TRIL:K_TRIL + 128] = (j <= i).astype(np.float32)
    return c


def relayout_w_in(w):
    w3 = w.reshape(8, 128, D_IN).transpose(1, 0, 2)
    def blk(k, c0, n):
        if k == "lqk":
            return np.concatenate([w3[:, :, C_LR:C_LR + 16], w3[:, :, C_Q:C_Q + 1024]], axis=2)
        return w3[:, :, c0:c0 + n]
    parts = [np.ascontiguousarray(blk(k, c0, n)).reshape(-1) for (k, c0, n) in WBLOCKS]
    return np.ascontiguousarray(np.concatenate(parts, axis=0)).reshape(128, 8 * D_IN)


def relayout_w_out(w):
    w4 = w.reshape(16, 128, 2, 512).transpose(2, 1, 0, 3)
    chunks = [w4[0], w4[1][:, 0:8], w4[1][:, 8:16]]
    return np.ascontiguousarray(np.concatenate([np.ascontiguousarray(c).reshape(-1) for c in chunks])).reshape(128, 16 * D)


def build_nc(ntiles=NT):
    nc = bass.Bass("TRN2", target_bir_lowering=False)
    seq = ntiles * 128
    x = nc.dram_tensor("x", [seq, D], F32, kind="ExternalInput").ap()
    gvd = nc.dram_tensor("gv", [128, 24], F32, kind="ExternalInput").ap()
    w_in = nc.dram_tensor("w_in", [128, 8 * D_IN], F32, kind="ExternalInput").ap()
    w_flat = w_in.rearrange("a b -> (a b)")
    w_a2 = nc.dram_tensor("w_a2", [17, 512], F32, kind="ExternalInput").ap()
    ln_b = nc.dram_tensor("ln_b", [1, D], F32, kind="ExternalInput").ap()
    w_s = nc.dram_tensor("w_s", [128, 1024], F32, kind="ExternalInput").ap()
    b_s = nc.dram_tensor("b_s", [1, 1024], F32, kind="ExternalInput").ap()
    w_out = nc.dram_tensor("w_out", [128, 16 * D], F32, kind="ExternalInput").ap()
    wo_flat = w_out.rearrange("a b -> (a b)")
    post_g = nc.dram_tensor("post_g", [1, D], F32, kind="ExternalInput").ap()
    consts = nc.dram_tensor("consts", [128, K_END], F32, kind="ExternalInput").ap()
    out = nc.dram_tensor("out", [seq, D], F32, kind="ExternalOutput").ap()

    S = Sched(nc)
    LNS = math.log(128.0 ** -0.5)
    sb = nc.alloc_sbuf_tensor
    wi = {k: sb("wi_" + k, [128, 8, n], BF16) for (k, c0, n) in WBLOCKS}
    wo = sb("wo", [128, 2, 16, 512], BF16)
    cst = sb("cst", [128, K_END], F32)
    identb = sb("identb", [128, 128], BF16)
    wTb = sb("wTb", [128, 8, 128], BF16)
    Qsb = sb("Qsb", [128, 8, 128], F32)
    pgB = sb("pgB", [128, D], F32)
    gv = sb("gv_sb", [128, 24], F32)
    gt = gv[:, 0:8]
    bng = gv[:, 8:16]
    lng = gv[:, 16:24]
    wa2x = sb("wa2x", [128, 512], BF16)
    mhalf = sb("mhalf", [128, 4], F32)
    ones1 = sb("ones1", [128, 1], F32)
    NX = 3
    xt = [sb(f"xt{i}", [128, D], F32) for i in range(NX)]
    hb = sb("hb", [128, D], BF16)
    hTs = [sb(f"hT{i}", [128, 8, 128], BF16) for i in range(2)]
    uT = sb("uT", [128, 8, 128], F32)
    sgT = sb("sgT", [128, 8, 128], F32)
    vg = sb("vg", [128, D], F32)
    vn = sb("vn", [128, D], BF16)
    sgB = sb("sgB", [128, D], F32)
    vb = sb("vb", [128, D], BF16)
    lrx = sb("lrx", [128, 128], BF16)
    la = sb("la", [128, 512], F32)
    E1 = sb("E1", [128, 512], F32)
    EX = sb("EX", [128, 4, 4], F32)
    EXr = sb("EXr", [128, 4, 4], F32)
    qeT = sb("qeT", [128, 4, 128], BF16)
    keT = sb("keT", [128, 4, 128], BF16)
    kdT = sb("kdT", [128, 4, 128], BF16)
    kd = sb("kd", [128, 512], BF16)
    scTb = sb("scTb", [128, 4, 128], BF16)
    St = sb("St", [128, 4, 256], F32)
    SbA = sb("SbA", [128, 4, 256], BF16)
    SbB = sb("SbB", [128, 4, 256], BF16)
    bo = sb("bo", [128, D], BF16)
    boT = hb[:, :].rearrange("p (c t) -> p c t", c=8)
    aoT = sb("aoT", [128, 8, 128], BF16)
    t2 = sb("t2", [128, D], F32)
    stat = sb("stat", [128, 40], F32)
    smask = sb("smask", [128, 512], BF16)
    bnst = sb("bnst", [128, 12], F32)
    pb = [nc.alloc_psum_tensor(f"pb{i}", [128, 512], F32) for i in range(4)]
    pq = nc.alloc_psum_tensor("pq", [128, 512], F32)
    pk = nc.alloc_psum_tensor("pk", [128, 512], F32)
    po = nc.alloc_psum_tensor("po", [128, 1024], F32)
    rc = [0]

    def rot():
        b = rc[0] % 4
        rc[0] += 1
        return pb[b], f"pb{b}"

    def PE(fn, r=(), w=(), p=()):
        return S.op("pe", fn, reads=r, writes=w, partial=p)

    def ACT(fn, r=(), w=(), p=()):
        return S.op("act", fn, reads=r, writes=w, partial=p)

    def DVE(fn, r=(), w=(), p=()):
        return S.op("dve", fn, reads=r, writes=w, partial=p)

    def POOL(fn, r=(), w=(), p=()):
        return S.op("pool", fn, reads=r, writes=w, partial=p)

    te, ve, se, ge = nc.tensor, nc.vector, nc.scalar, nc.gpsimd

    wnat = t2
    lb1 = sgT
    rsbs = vg
    lb1f = lb1[:, :, :].rearrange("p g t -> p (g t)")
    lns_t = sb("lns_t", [128, 1], F32)
    POOL(lambda: ge.memset(lns_t[:, :], LNS), w=["lns"])
    POOL(lambda: ge.memset(mhalf[:, :], -0.5), w=["mhalf"])
    POOL(lambda: ge.memset(smask[:, :], 1.0), w=["smask"])
    DVE(lambda: ve.memset(smask[:, :].rearrange("p (c t) -> p c t", t=64)[:, :, 0:1], 0.0), p=["smask"])
    POOL(lambda: ge.memset(ones1[:, :], 1.0), w=["ones1"])
    POOL(lambda: ge.memset(lrx[:, :], 1.0), w=["lrx"])
    POOL(lambda: ge.memset(wa2x[:, :], 0.0), w=["wa2x"])
    POOL(lambda: ge.memset(St[:, :, :], 0.0), w=["St"])
    POOL(lambda: ge.memset(lb1f[0:2, :], 1.0), w=["sgT"])
    S.dma("sp", pgB[:, :], post_g.partition_broadcast(128), writes=["pgB"], semkey="pgB")
    S.dma("sp", xt[0][:, :], x[0:128, :], writes=["xt0"], semkey="xt0")
    S.dma("sp", gv[:, :], gvd[:, :], writes=["gt", "bng", "lng"], semkey="gv")
    S.dma("sp", cst[:, :], consts[:, :], writes=["cst"], semkey="cst")
    S.dma("sp", wnat[:, :], w_s[:, :], writes=["t2"], semkey="wnat")
    S.dma("sp", E1[0:17, :], w_a2[:, :], writes=["E1a"], semkey="wa2x")
    S.dma("sp", rsbs[1:2, :], b_s[:, :], writes=["vg"], semkey="bs")
    S.dma("sp", lb1f[0:1, :], ln_b[:, :], partial=["sgT"], semkey="lnb")
    DVE(lambda: ve.tensor_copy(out=wa2x[0:17, :], in_=E1[0:17, :]), r=["E1a"], w=["E1"], p=["wa2x"])
    small = ["cst", "t2", "xt0", "gt", "bng", "lng", "vg", "sgT", "E1a", "pgB"]
    first_w = [True]
    off = 0
    for (k, c0, n) in WBLOCKS:
        S.dma("pool", wi[k][:, :, :].rearrange("p c n -> p (c n)"), w_flat[off * 128:(off + 8 * n) * 128].rearrange("(p m) -> p m", p=128), writes=["wi_" + k], semkey="wi_" + k,
              reads=small if first_w[0] else (), max_dma_last_dim=8192)
        first_w[0] = False
        off += 8 * n
    S.dma("pool", wo[:, 0, :, :].rearrange("p c n -> p (c n)"), wo_flat[0:8192 * 128].rearrange("(p m) -> p m", p=128), writes=["wo0a", "wo0b"], semkey="wo0",
          max_dma_last_dim=8192)
    for q in range(2):
        S.dma("pool", wo[:, 1, q * 8:(q + 1) * 8, :].rearrange("p c n -> p (c n)"),
              wo_flat[(8192 + q * 4096) * 128:(8192 + (q + 1) * 4096) * 128].rearrange("(p m) -> p m", p=128), writes=["wo1" + "ab"[q]], semkey=f"wo1{q}",
              max_dma_last_dim=8192)

    DVE(lambda: ve.tensor_copy(out=identb[:, :], in_=cst[:, K_ID:K_ID + 128]), r=["cst"], w=["identb"])
    pA, tA = rot()
    pB, tB = rot()
    for g in range(8):
        pbk, tk = (pA, tA) if g < 4 else (pB, tB)
        PE(lambda g=g, pbk=pbk: te.transpose(out=pbk[:, (g % 4) * 128:(g % 4 + 1) * 128], in_=wnat[:, g * 128:(g + 1) * 128],
                                             identity=cst[:, K_ID:K_ID + 128]),
           r=["t2", "cst"], w=[tk] if g % 4 == 0 else (), p=() if g % 4 == 0 else [tk])
    wTm = uT
    for g in range(8):
        pbk, tk = (pA, tA) if g < 4 else (pB, tB)
        DVE(lambda g=g, pbk=pbk: ve.tensor_tensor(out=wTm[:, g, :], in0=pbk[:, (g % 4) * 128:(g % 4 + 1) * 128],
                                                  in1=cst[:, K_TRIL:K_TRIL + 128], op=ALU.mult),
            r=[tk, "cst"], p=["uT"])
    DVE(lambda: ve.tensor_copy(out=wTb[:, :, :], in_=wTm[:, :, :]), r=["uT"], w=["wTb"])
    pR0, tR0 = rot()
    pR1, tR1 = rot()
    wTm2 = wTm[:, :, :].rearrange("p g t -> p (g t)")
    PE(lambda: te.matmul(pR0[0:1, :], lhsT=ones1[:, 0:1], rhs=wTm2[:, 0:512], start=True, stop=True), r=["ones1", "uT"], w=[tR0])
    PE(lambda: te.matmul(pR1[0:1, :], lhsT=ones1[:, 0:1], rhs=wTm2[:, 512:1024], start=True, stop=True), r=["ones1", "uT"], w=[tR1])
    DVE(lambda: ve.tensor_copy(out=rsbs[0:1, 0:512], in_=pR0[0:1, :]), r=[tR0], p=["vg"])
    DVE(lambda: ve.tensor_copy(out=rsbs[0:1, 512:1024], in_=pR1[0:1, :]), r=[tR1], p=["vg"])
    pQ0, tQ0 = rot()
    pQ1, tQ1 = rot()
    for g in range(8):
        pbk, tk = (pQ0, tQ0) if g < 4 else (pQ1, tQ1)
        PE(lambda g=g, pbk=pbk: te.matmul(pbk[:, (g % 4) * 128:(g % 4 + 1) * 128], lhsT=lb1f[0:2, g * 128:(g + 1) * 128],
                                          rhs=rsbs[0:2, g * 128:(g + 1) * 128], start=True, stop=True),
           r=["sgT", "vg"], w=[tk] if g % 4 == 0 else (), p=() if g % 4 == 0 else [tk])
    Qf = Qsb[:, :, :].rearrange("p g t -> p (g t)")
    DVE(lambda: ve.tensor_copy(out=Qf[:, 0:512], in_=pQ0[:, :]), r=[tQ0], p=["Qsb"])
    DVE(lambda: ve.tensor_copy(out=Qf[:, 512:1024], in_=pQ1[:, :]), r=[tQ1], p=["Qsb"])


    def load_x(t):
        S.dma("sp", xt[t % NX][:, :], x[t * 128:(t + 1) * 128, :], writes=[f"xt{t % NX}"], semkey=f"xt{t % NX}")

    def stage_P_elem(t):
        xs = xt[t % NX]
        xk = f"xt{t % NX}"
        ACT(lambda: se.activation(out=hb[:, :], in_=xs[:, :], func=AF.Square, accum_out=stat[:, 0:1]),
            r=[xk], w=["hb", "st_ss"])
        DVE(lambda: ve.tensor_scalar(out=stat[:, 1:2], in0=stat[:, 0:1], scalar1=1.0 / D, scalar2=EPS, op0=ALU.mult, op1=ALU.add),
            r=["st_ss"], w=["st_ms"])
        POOL(lambda: ge.tensor_tensor(out=stat[:, 2:3], in0=stat[:, 1:2], in1=mhalf[:, 0:1], op=ALU.pow),
             r=["st_ms", "mhalf"], w=["st_rstd"])
        DVE(lambda: ve.tensor_scalar(out=hb[:, :], in0=xs[:, :], scalar1=stat[:, 2:3], scalar2=None, op0=ALU.mult),
            r=[xk, "st_rstd"], w=["hb"])

    def stage_P_pe(t):
        hT = hTs[t % 2]
        hk = f"hT{t % 2}"
        pt, tk = rot()
        ptb = pt[:, :].bitcast(BF16).rearrange("p (c t) -> p c t", c=8)
        for c in range(8):
            PE(lambda c=c: te.transpose(out=ptb[:, c, :], in_=hb[:, c * 128:(c + 1) * 128], identity=identb[:, :]),
               r=["hb", "identb"], w=[tk] if c == 0 else (), p=() if c == 0 else [tk])
        DVE(lambda: ve.tensor_tensor(out=hT[:, :, :], in0=ptb, in1=gt.unsqueeze(2).to_broadcast([128, 8, 128]), op=ALU.mult),
            r=[tk, "gt"], w=[hk])

    def tok_group(t, col0, wkey):
        hT = hTs[t % 2]
        hk = f"hT{t % 2}"
        pt, tk = rot()
        for c in range(8):
            PE(lambda c=c: te.matmul(pt[:, :], lhsT=hT[:, c, :], rhs=wi[wkey][:, c, :], start=(c == 0), stop=(c == 7)),
               r=[hk, "wi_" + wkey], w=[tk] if c == 0 else (), p=() if c == 0 else [tk])
        return pt, tk

    def feat_group(t, col0, wkey, pt=None, tk=None):
        hT = hTs[t % 2]
        hk = f"hT{t % 2}"
        if pt is None:
            pt, tk = rot()
        first = True
        for j in range(4):
            for c in range(8):
                PE(lambda c=c, j=j: te.matmul(pt[:, j * 128:(j + 1) * 128], lhsT=wi[wkey][:, c, col0 + j * 128:col0 + (j + 1) * 128],
                                              rhs=hT[:, c, :], start=(c == 0), stop=(c == 7)),
                   r=[hk, "wi_" + wkey], w=[tk] if first else (), p=() if first else [tk])
                first = False
        return pt, tk

    def stage_head(t):
        hT = hTs[t % 2]
        hk = f"hT{t % 2}"
        plr, tlr = rot()
        for c in range(8):
            PE(lambda c=c: te.matmul(plr[:, 0:128], lhsT=wi["lqk"][:, c, 0:128], rhs=hT[:, c, :], start=(c == 0), stop=(c == 7)),
               r=[hk, "wi_lqk"], w=[tlr] if c == 0 else (), p=() if c == 0 else [tlr])
        ACT(lambda: se.copy(out=lrx[0:16, :], in_=plr[0:16, 0:128]), r=[tlr], p=["lrx"])
        feat_group(t, 16, "lqk", pq, "pq")
        feat_group(t, 16 + 512, "lqk", pk, "pk")

    def stage_gate(t):
        rot()
        rot()
        pgl, tgl = rot()
        for h in range(4):
            PE(lambda h=h: te.matmul(pgl[:, h * 128:(h + 1) * 128], lhsT=wa2x[:, h * 128:(h + 1) * 128], rhs=lrx[:, :], start=True, stop=True),
               r=["lrx", "wa2x"], w=[tgl] if h == 0 else (), p=() if h == 0 else [tgl])
        ACT(lambda: se.activation(out=la[:, :], in_=pgl[:, :], func=AF.Exp, scale=-1.0), r=[tgl], w=["la"])
        ACT(lambda: se.activation(out=la[:, :], in_=la[:, :], func=AF.Ln, bias=1.0), r=["la"], w=["la"])
        cc = vg[:, 512:1024]
        cc3 = cc.rearrange("p (c t) -> p c t", t=64)
        DVE(lambda: ve.tensor_tensor_scan(out=cc, data0=smask[:, :], data1=la[:, :], initial=0.0, op0=ALU.mult, op1=ALU.add),
            r=["smask", "la"], p=["vg"])
        DVE(lambda: ve.tensor_tensor(out=E1[:, :].rearrange("p (c t) -> p c t", t=64), in0=cc3, in1=cc3[:, :, 31:32].to_broadcast([128, 8, 64]), op=ALU.subtract),
            r=["vg"], w=["E1"])
        DVE(lambda: ve.tensor_tensor(out=la[:, :].rearrange("p (c t) -> p c t", t=64), in0=cc3[:, :, 63:64].to_broadcast([128, 8, 64]), in1=cc3, op=ALU.subtract),
            r=["vg"], w=["la"])
        cc4 = cc.rearrange("p (h c t) -> p h c t", h=4, c=2)
        DVE(lambda: ve.tensor_copy(out=EXr[:, :, 0:2], in_=cc4[:, :, :, 31]), r=["vg"], w=["EXr"])
        DVE(lambda: ve.tensor_copy(out=EXr[:, :, 2:4], in_=cc4[:, :, :, 63]), r=["vg"], p=["EXr"])

    def iteration(t):
        xs = xt[t % NX]
        xk = f"xt{t % NX}"
        if t + 2 < ntiles:
            load_x(t + 2)
        ACT(lambda: se.activation(out=vg[:, 0:512], in_=E1[:, :], func=AF.Exp, scale=1.0 / 16), r=["E1"], p=["vg"])
        ACT(lambda: se.activation(out=E1[:, :], in_=E1[:, :], func=AF.Exp, scale=-1.0 / 16, bias=lns_t[:, 0:1]), r=["E1", "lns"], w=["E1"])
        ACT(lambda: se.activation(out=la[:, :], in_=la[:, :], func=AF.Exp, scale=-1.0 / 16), r=["la"], w=["la"])
        ACT(lambda: se.activation(out=EX[:, :, :].rearrange("p h k -> p (h k)"), in_=EXr[:, :, :].rearrange("p h k -> p (h k)"), func=AF.Exp, scale=-1.0 / 16),
            r=["EXr"], w=["EX"])
        for hh in range(2):
            pt, tk = tok_group(t, C_V + hh * 512, f"v{hh}")
            DVE(lambda pt=pt, hh=hh: ve.tensor_copy(out=vb[:, hh * 512:(hh + 1) * 512], in_=pt[:, :]), r=[tk],
                w=["vb"] if hh == 0 else (), p=() if hh == 0 else ["vb"])
        DVE(lambda: ve.tensor_tensor(out=qeT[:, :, :].rearrange("p h t -> p (h t)"), in0=pq[:, :], in1=E1[:, :], op=ALU.mult),
            r=["pq", "E1"], w=["qeT"])
        DVE(lambda: ve.tensor_tensor(out=keT[:, :, :].rearrange("p h t -> p (h t)"), in0=pk[:, :], in1=vg[:, 0:512], op=ALU.mult),
            r=["pk", "vg"], w=["keT"])
        DVE(lambda: ve.tensor_tensor(out=kdT[:, :, :].rearrange("p h t -> p (h t)"), in0=pk[:, :], in1=la[:, :], op=ALU.mult),
            r=["pk", "la"], w=["kdT"])
        for h in range(4):
            POOL(lambda h=h: ge.tensor_scalar(out=SbA[:, h, :], in0=St[:, h, :], scalar1=EX[:, h, 0:1], scalar2=1.0, op0=ALU.mult, op1=ALU.mult),
                 r=["St", "EX"], w=["SbA"] if h == 0 else (), p=() if h == 0 else ["SbA"])
        if t + 1 < ntiles:
            stage_P_elem(t + 1)
        for hh in range(2):
            pt, tk = tok_group(t, C_AV + hh * 512, f"av{hh}")
            ACT(lambda pt=pt, hh=hh: se.activation(out=vg[:, hh * 512:(hh + 1) * 512], in_=pt[:, :], func=AF.Gelu), r=[tk],
                w=["vg"] if hh == 0 else (), p=() if hh == 0 else ["vg"])
        for hh in range(2):
            pt, tk = feat_group(t, 0, f"au{hh}")
            ACT(lambda pt=pt, hh=hh: se.activation(out=uT[:, hh * 4:(hh + 1) * 4, :].rearrange("p g t -> p (g t)"), in_=pt[:, :], func=AF.Gelu),
                r=[tk], w=["uT"] if hh == 0 else (), p=() if hh == 0 else ["uT"])
        DVE(lambda: ve.bn_stats(out=bnst[:, 0:6], in_=vg[:, 0:512]), r=["vg"], w=["bnst"])
        DVE(lambda: ve.bn_stats(out=bnst[:, 6:12], in_=vg[:, 512:1024]), r=["vg"], p=["bnst"])
        DVE(lambda: ve.bn_aggr(out=stat[:, 8:10], in_=bnst[:, :]), r=["bnst"], w=["st_mv"])
        DVE(lambda: ve.tensor_scalar(out=stat[:, 10:11], in0=stat[:, 9:10], scalar1=EPS, scalar2=None, op0=ALU.add), r=["st_mv"], w=["st_ve"])
        POOL(lambda: ge.tensor_tensor(out=stat[:, 11:12], in0=stat[:, 10:11], in1=mhalf[:, 0:1], op=ALU.pow), r=["st_ve", "mhalf"], w=["st_lr"])
        pkt, tkt = rot()
        pktb = pkt[:, :].bitcast(BF16)
        for h in range(4):
            PE(lambda h=h: te.transpose(out=pktb[:, h * 128:(h + 1) * 128], in_=kdT[:, h, :], identity=identb[:, :]),
               r=["kdT", "identb"], w=[tkt] if h == 0 else (), p=() if h == 0 else [tkt])
        DVE(lambda: ve.tensor_copy(out=kd[:, :], in_=pktb[:, 0:512]), r=[tkt], w=["kd"])
        psc, tsc = rot()
        for h in range(4):
            PE(lambda h=h: te.matmul(psc[:, h * 128:(h + 1) * 128], lhsT=keT[:, h, :], rhs=qeT[:, h, :], start=True, stop=True),
               r=["keT", "qeT"], w=[tsc] if h == 0 else (), p=() if h == 0 else [tsc])
        DVE(lambda: ve.tensor_tensor(out=scTb[:, :, :].rearrange("p h t -> p (h t)"), in0=psc[:, :], in1=cst[:, K_MASK:K_MASK + 512], op=ALU.mult),
            r=[tsc, "cst"], w=["scTb"])
        DVE(lambda: ve.scalar_tensor_tensor(out=stat[:, 12:13], in0=stat[:, 8:9], scalar=-1.0, in1=stat[:, 11:12], op0=ALU.mult, op1=ALU.mult),
            r=["st_mv", "st_lr"], w=["st_nmr"])
        ACT(lambda: se.activation(out=vn[:, :], in_=vg[:, :], func=AF.Identity, scale=stat[:, 11:12], bias=stat[:, 12:13]),
            r=["vg", "st_lr", "st_nmr"], w=["vn"])
        for hh in range(2):
            pt, tk = tok_group(t, C_BG + hh * 512, f"bg{hh}")
            ACT(lambda pt=pt, hh=hh: se.activation(out=sgB[:, hh * 512:(hh + 1) * 512], in_=pt[:, :], func=AF.Silu), r=[tk],
                w=["sgB"] if hh == 0 else (), p=() if hh == 0 else ["sgB"])
        for h in range(4):
            hc = slice(h * 256, (h + 1) * 256)
            PE(lambda h=h, hc=hc: te.matmul(po[:, hc], lhsT=scTb[:, h, :], rhs=vb[:, hc], start=(h % 2 == 0), stop=False, skip_group_check=True),
               r=["scTb", "vb"], w=["po"] if h == 0 else (), p=() if h == 0 else ["po"])
        for h in range(4):
            hc = slice(h * 256, (h + 1) * 256)
            PE(lambda h=h, hc=hc: te.matmul(po[0:64, hc], lhsT=qeT[:, h, 0:64], rhs=SbA[:, h, :], start=False, stop=False, skip_group_check=True),
               r=["qeT", "SbA"], p=["po"])
        pkvs = [(pq, "pq"), (pk, "pk")]
        for h in range(4):
            pkv, tkv = pkvs[h // 2]
            PE(lambda h=h, pkv=pkv: te.matmul(pkv[:, (h % 2) * 256:(h % 2 + 1) * 256], lhsT=kd[0:64, h * 128:(h + 1) * 128], rhs=vb[0:64, h * 256:(h + 1) * 256],
                                              start=True, stop=True),
               r=["kd", "vb"], w=[tkv] if h % 2 == 0 else (), p=() if h % 2 == 0 else [tkv])
        for h in range(4):
            pkv, tkv = pkvs[h // 2]
            DVE(lambda h=h, pkv=pkv: ve.scalar_tensor_tensor(out=St[:, h, :], in0=St[:, h, :], scalar=EX[:, h, 2:3], in1=pkv[:, (h % 2) * 256:(h % 2 + 1) * 256],
                                                             op0=ALU.mult, op1=ALU.add),
                r=["St", "EX", tkv], p=["St"])
        for h in range(4):
            POOL(lambda h=h: ge.tensor_scalar(out=SbB[:, h, :], in0=St[:, h, :], scalar1=EX[:, h, 1:2], scalar2=1.0, op0=ALU.mult, op1=ALU.mult),
                 r=["St", "EX"], w=["SbB"] if h == 0 else (), p=() if h == 0 else ["SbB"])
        psp = [rot(), rot()]
        for g in range(8):
            pt, tk = psp[g // 4]
            PE(lambda g=g, pt=pt: te.matmul(pt[:, (g % 4) * 128:(g % 4 + 1) * 128], lhsT=vn[:, g * 128:(g + 1) * 128], rhs=wTb[:, g, :], start=True, stop=True),
               r=["vn", "wTb"], w=[tk] if g % 4 == 0 else (), p=() if g % 4 == 0 else [tk])
        t1 = t2[:, :].rearrange("p (g t) -> p g t", g=8)
        for g in range(8):
            pt, tk = psp[g // 4]
            DVE(lambda g=g, pt=pt: ve.scalar_tensor_tensor(out=t1[:, g, :], in0=pt[:, (g % 4) * 128:(g % 4 + 1) * 128], scalar=lng[:, g:g + 1], in1=Qsb[:, g, :],
                                                           op0=ALU.mult, op1=ALU.add),
                r=[tk, "lng", "Qsb"], w=["t2"] if g == 0 else (), p=() if g == 0 else ["t2"])
        POOL(lambda: ge.tensor_tensor(out=uT[:, :, :], in0=uT[:, :, :], in1=t1, op=ALU.mult), r=["uT", "t2"], w=["uT"])
        for hh in range(2):
            pt, tk = feat_group(t, 0, f"ag{hh}")
            ACT(lambda pt=pt, hh=hh: se.activation(out=sgT[:, hh * 4:(hh + 1) * 4, :].rearrange("p g t -> p (g t)"), in_=pt[:, :], func=AF.Silu),
                r=[tk], w=["sgT"] if hh == 0 else (), p=() if hh == 0 else ["sgT"])
        if t + 1 < ntiles:
            stage_P_pe(t + 1)
        for h in range(4):
            hc = slice(h * 256, (h + 1) * 256)
            PE(lambda h=h, hc=hc: te.matmul(po[64:128, hc], lhsT=qeT[:, h, 64:128], rhs=SbB[:, h, :], start=False, stop=True, skip_group_check=True),
               r=["qeT", "SbB"], p=["po"])
        pkvs = [rot(), rot()]
        for h in range(4):
            pkv, tkv = pkvs[h // 2]
            PE(lambda h=h, pkv=pkv: te.matmul(pkv[:, (h % 2) * 256:(h % 2 + 1) * 256], lhsT=kd[64:128, h * 128:(h + 1) * 128], rhs=vb[64:128, h * 256:(h + 1) * 256],
                                              start=True, stop=True),
               r=["kd", "vb"], w=[tkv] if h % 2 == 0 else (), p=() if h % 2 == 0 else [tkv])
        for h in range(4):
            ACT(lambda h=h: se.activation(out=bo[:, h * 256:(h + 1) * 256], in_=po[:, h * 256:(h + 1) * 256], func=AF.Square, accum_out=stat[:, 16 + h:17 + h]),
                r=["po"], w=["bo", "st_so"] if h == 0 else (), p=() if h == 0 else ["bo", "st_so"])
        DVE(lambda: ve.tensor_scalar(out=stat[:, 20:24], in0=stat[:, 16:20], scalar1=1.0 / 256, scalar2=EPS, op0=ALU.mult, op1=ALU.add),
            r=["st_so"], w=["st_mo"])
        POOL(lambda: ge.tensor_tensor(out=stat[:, 24:28], in0=stat[:, 20:24], in1=mhalf[:, 0:4], op=ALU.pow), r=["st_mo", "mhalf"], w=["st_ro"])
        for h in range(4):
            DVE(lambda h=h: ve.scalar_tensor_tensor(out=bo[:, h * 256:(h + 1) * 256], in0=po[:, h * 256:(h + 1) * 256], scalar=stat[:, 24 + h:25 + h],
                                                    in1=sgB[:, h * 256:(h + 1) * 256], op0=ALU.mult, op1=ALU.mult),
                r=["po", "st_ro", "sgB"], w=["bo"] if h == 0 else (), p=() if h == 0 else ["bo"])
        for h in range(4):
            pkv, tkv = pkvs[h // 2]
            DVE(lambda h=h, pkv=pkv: ve.scalar_tensor_tensor(out=St[:, h, :], in0=St[:, h, :], scalar=EX[:, h, 3:4], in1=pkv[:, (h % 2) * 256:(h % 2 + 1) * 256],
                                                             op0=ALU.mult, op1=ALU.add),
                r=["St", "EX", tkv], p=["St"])
        POOL(lambda: ge.tensor_tensor(out=aoT[:, :, :], in0=uT[:, :, :], in1=sgT[:, :, :], op=ALU.mult), r=["uT", "sgT"], w=["aoT"])
        if t + 1 < ntiles:
            stage_head(t + 1)
        pbt, tbt = rot()
        pbtb = pbt[:, :].bitcast(BF16).rearrange("p (c t) -> p c t", c=8)
        for c in range(8):
            PE(lambda c=c: te.transpose(out=pbtb[:, c, :], in_=bo[:, c * 128:(c + 1) * 128], identity=identb[:, :]),
               r=["bo", "identb"], w=[tbt] if c == 0 else (), p=() if c == 0 else [tbt])
        DVE(lambda: ve.tensor_tensor(out=boT, in0=pbtb, in1=bng.unsqueeze(2).to_broadcast([128, 8, 128]), op=ALU.mult),
            r=[tbt, "bng"], w=["hb"])
        if t + 1 < ntiles:
            stage_gate(t + 1)
        if t == 0:
            order = [(hh, fc) for hh in range(2) for fc in range(16)]
        else:
            order = [(hh, fc) for half in range(2) for hh in range(2) for fc in range(half * 8, half * 8 + 8)]
        for (hh, fc) in order:
            if True:
                lh = aoT[:, fc, :] if fc < 8 else boT[:, fc - 8, :]
                PE(lambda hh=hh, fc=fc, lh=lh: te.matmul(po[:, hh * 512:(hh + 1) * 512], lhsT=lh, rhs=wo[:, hh, fc, :],
                                                         start=(fc == 0), stop=(fc == 15)),
                   r=["aoT" if fc < 8 else "hb", f"wo{hh}" + ("a" if fc < 8 else "b")], w=["po"] if (hh == 0 and fc == 0) else (), p=() if (hh == 0 and fc == 0) else ["po"])
        ACT(lambda: se.activation(out=t2[:, :], in_=po[:, :], func=AF.Square, accum_out=stat[:, 32:33]), r=["po"], w=["t2", "st_sy"])
        DVE(lambda: ve.tensor_scalar(out=stat[:, 33:34], in0=stat[:, 32:33], scalar1=1.0 / D, scalar2=EPS, op0=ALU.mult, op1=ALU.add),
            r=["st_sy"], w=["st_my"])
        POOL(lambda: ge.tensor_tensor(out=stat[:, 34:35], in0=stat[:, 33:34], in1=mhalf[:, 0:1], op=ALU.pow), r=["st_my", "mhalf"], w=["st_ry"])
        DVE(lambda: ve.scalar_tensor_tensor(out=t2[:, :], in0=po[:, :], scalar=stat[:, 34:35], in1=pgB[:, :], op0=ALU.mult, op1=ALU.mult),
            r=["po", "st_ry", "pgB"], w=["t2"])
        POOL(lambda: ge.tensor_tensor(out=xs[:, :], in0=xs[:, :], in1=t2[:, :], op=ALU.add), r=[xk, "t2"], w=[xk])
        S.dma("sp", out[t * 128:(t + 1) * 128, :], xs[:, :], reads=[xk], semkey="o_" + xk)

    if ntiles > 1:
        load_x(1)
    stage_P_elem(0)
    stage_P_pe(0)
    stage_head(0)
    stage_gate(0)
    for t in range(ntiles):
        iteration(t)
    stats = S.emit()
    return nc, stats


_CACHE = {}


def kernel(x, pre_norm_g, w_in, w_a2, b_a2, a_ln_g, a_ln_b, a_w_s, a_b_s, b_norm_g, w_out, post_norm_g):
    f = lambda a: np.ascontiguousarray(np.asarray(a, dtype=np.float32))
    pc = lambda v: np.ascontiguousarray(v.reshape(8, 128).T)
    x = f(x)
    B = x.shape[0]
    if "nc" not in _CACHE:
        _CACHE["nc"] = build_nc()
    nc, stats = _CACHE["nc"]
    shared = dict(
        gv=np.ascontiguousarray(np.concatenate([pc(f(pre_norm_g)[0]), pc(f(b_norm_g)[0]), pc(f(a_ln_g)[0])], axis=1)), w_in=relayout_w_in(f(w_in)[0]), w_a2=np.ascontiguousarray(np.concatenate([f(w_a2)[0], f(b_a2)[0].reshape(1, 512)], axis=0)),
         ln_b=f(a_ln_b)[0].reshape(1, D), w_s=np.ascontiguousarray(f(a_w_s)[0].transpose(1, 0, 2)).reshape(128, 1024), b_s=f(a_b_s)[0].reshape(1, 1024),
         w_out=relayout_w_out(f(w_out)[0]), post_g=f(post_norm_g)[0].reshape(1, D), consts=host_consts(),
    )
    in_maps = [dict(shared, x=x[i]) for i in range(B)]
    res = run_bass_kernel_spmd(nc, in_maps, core_ids=list(range(B)))
    return np.stack([r["out"] for r in res.results], axis=0).astype(np.float32)
```

```python
import math
import numpy as np
import concourse.bass as bass
import concourse.mybir as mybir
from concourse.bass_utils import run_bass_kernel_spmd

F32 = mybir.dt.float32
BF16 = mybir.dt.bfloat16
AF = mybir.ActivationFunctionType
ALU = mybir.AluOpType

D = 1024
SEQ = 2048
NT = SEQ // 128
D_IN = 6160
EPS = 1e-6
C_AU, C_AV, C_AG, C_Q, C_K, C_V, C_BG, C_LR = 0, 1024, 2048, 3072, 3584, 4096, 5120, 6144
WBLOCKS = [("lqk", C_Q, 1040), ("v0", C_V, 512), ("v1", C_V + 512, 512),
           ("av0", C_AV, 512), ("av1", C_AV + 512, 512), ("au0", C_AU, 512), ("au1", C_AU + 512, 512),
           ("bg0", C_BG, 512), ("bg1", C_BG + 512, 512), ("ag0", C_AG, 512), ("ag1", C_AG + 512, 512)]
K_ID, K_MASK, K_TRIL, K_END = 0, 128, 640, 768


class Sched:
    def __init__(self, nc):
        self.nc = nc
        self.eng = {"pe": nc.tensor, "act": nc.scalar, "dve": nc.vector,
                    "pool": nc.gpsimd, "sp": nc.sync}
        self.ops = []
        self.writers = {}
        self.readers = {}

    def _reduce(self, idxs):
        best = {}
        out = []
        for i in idxs:
            o = self.ops[i]
            if o["dma"]:
                out.append(i)
            else:
                e = o["eng"]
                if e not in best or best[e] < i:
                    best[e] = i
        return sorted(set(out) | set(best.values()))

    def op(self, eng, fn, reads=(), writes=(), partial=(), dma=False, semkey=None):
        idx = len(self.ops)
        raw = set()
        other = set()
        for b in reads:
            raw.update(self.writers.get(b, ()))
        for b in list(writes) + list(partial):
            other.update(self.writers.get(b, ()))
            other.update(self.readers.get(b, ()))
        self.ops.append(dict(eng=eng, fn=fn, raw=raw, other=other - raw, dma=dma, semkey=semkey))
        for b in reads:
            self.readers[b] = self._reduce(list(self.readers.get(b, [])) + [idx])
        for b in writes:
            self.writers[b] = [idx]
            self.readers[b] = []
        for b in partial:
            self.writers[b] = self._reduce(list(self.writers.get(b, [])) + [idx])
        return idx

    def dma(self, queue, out, in_, reads=(), writes=(), partial=(), semkey=None, **kw):
        eng = self.eng[queue]
        return self.op(queue, lambda: eng.dma_start(out=out, in_=in_, **kw), reads=reads,
                       writes=writes, partial=partial, dma=True, semkey=semkey)

    def emit(self, final_wait_eng="sp"):
        nc = self.nc
        ops = self.ops
        need = []
        signal = [False] * len(ops)
        for i, o in enumerate(ops):
            e = o["eng"]
            deps = set()
            for d in o["raw"]:
                od = ops[d]
                if od["dma"] or od["eng"] != e or e != "pe":
                    deps.add(d)
            for d in o["other"]:
                od = ops[d]
                if od["dma"] or od["eng"] != e:
                    deps.add(d)
            deps = self._reduce(deps)
            need.append(deps)
            for d in deps:
                signal[d] = True
        for i, o in enumerate(ops):
            if o["dma"]:
                signal[i] = True
        sems = {}

        def getsem(name):
            if name not in sems:
                sems[name] = nc.alloc_semaphore("s_" + "_".join(str(x) for x in name))
            return sems[name]

        cnt = {}
        sig = {}
        for i, o in enumerate(ops):
            if not signal[i]:
                continue
            if o["dma"]:
                key = ("dma", o["semkey"] if o["semkey"] is not None else i)
                cnt[key] = cnt.get(key, 0) + 16
            else:
                key = ("eng", o["eng"])
                cnt[key] = cnt.get(key, 0) + 1
            sig[i] = (key, cnt[key])
        seen = {e: {} for e in self.eng}
        nwaits = 0
        for i, o in enumerate(ops):
            e = o["eng"]
            engine = self.eng[e]
            for d in need[i]:
                key, val = sig[d]
                if seen[e].get(key, 0) >= val:
                    continue
                seen[e][key] = val
                engine.wait_ge(getsem(key), val)
                nwaits += 1
            ins = o["fn"]()
            if signal[i]:
                key, val = sig[i]
                ins.then_inc(getsem(key), 16 if o["dma"] else 1)
        engine = self.eng[final_wait_eng]
        for key, val in cnt.items():
            if key[0] == "dma":
                if seen[final_wait_eng].get(key, 0) >= val:
                    continue
                engine.wait_ge(getsem(key), val)
        self.stats = dict(n_ops=len(ops), n_waits=nwaits, n_sems=len(sems),
                          per_eng={e: sum(1 for o in ops if o["eng"] == e) for e in self.eng})
        return self.stats


def host_consts():
    c = np.zeros((128, K_END), np.float32)
    c[:, K_ID:K_ID + 128] = np.eye(128, dtype=np.float32)
    j = np.arange(128)[:, None]
    i = np.arange(128)[None, :]
    same = (j // 64) == (i // 64)
    mask = (same & (j <= i)).astype(np.float32)
    c[:, K_MASK:K_MASK + 512] = np.tile(mask, (1, 4))
    c[:, K_TRIL:K_TRIL + 128] = (j <= i).astype(np.float32)
    return c


def relayout_w_in(w):
    w3 = w.reshape(8, 128, D_IN).transpose(1, 0, 2)
    def blk(k, c0, n):
        if k == "lqk":
            return np.concatenate([w3[:, :, C_LR:C_LR + 16], w3[:, :, C_Q:C_Q + 1024]], axis=2)
        return w3[:, :, c0:c0 + n]
    parts = [np.ascontiguousarray(blk(k, c0, n)).reshape(-1) for (k, c0, n) in WBLOCKS]
    return np.ascontiguousarray(np.concatenate(parts, axis=0)).reshape(128, 8 * D_IN)


def relayout_w_out(w):
    w4 = w.reshape(16, 128, 2, 512).transpose(2, 1, 0, 3)
    chunks = [w4[0], w4[1][:, 0:8], w4[1][:, 8:16]]
    return np.ascontiguousarray(np.concatenate([np.ascontiguousarray(c).reshape(-1) for c in chunks])).reshape(128, 16 * D)


def build_nc(ntiles=NT):
    nc = bass.Bass("TRN2", target_bir_lowering=False)
    seq = ntiles * 128
    x = nc.dram_tensor("x", [seq, D], F32, kind="ExternalInput").ap()
    gvd = nc.dram_tensor("gv", [128, 24], F32, kind="ExternalInput").ap()
    w_in = nc.dram_tensor("w_in", [128, 8 * D_IN], F32, kind="ExternalInput").ap()
    w_flat = w_in.rearrange("a b -> (a b)")
    w_a2 = nc.dram_tensor("w_a2", [17, 512], F32, kind="ExternalInput").ap()
    ln_b = nc.dram_tensor("ln_b", [1, D], F32, kind="ExternalInput").ap()
    w_s = nc.dram_tensor("w_s", [128, 1024], F32, kind="ExternalInput").ap()
    b_s = nc.dram_tensor("b_s", [1, 1024], F32, kind="ExternalInput").ap()
    w_out = nc.dram_tensor("w_out", [128, 16 * D], F32, kind="ExternalInput").ap()
    wo_flat = w_out.rearrange("a b -> (a b)")
    post_g = nc.dram_tensor("post_g", [1, D], F32, kind="ExternalInput").ap()
    consts = nc.dram_tensor("consts", [128, K_END], F32, kind="ExternalInput").ap()
    out = nc.dram_tensor("out", [seq, D], F32, kind="ExternalOutput").ap()

    S = Sched(nc)
    LNS = math.log(128.0 ** -0.5)
    sb = nc.alloc_sbuf_tensor
    wi = {k: sb("wi_" + k, [128, 8, n], BF16) for (k, c0, n) in WBLOCKS}
    wo = sb("wo", [128, 2, 16, 512], BF16)
    cst = sb("cst", [128, K_END], F32)
    identb = sb("identb", [128, 128], BF16)
    wTb = sb("wTb", [128, 8, 128], BF16)
    Qsb = sb("Qsb", [128, 8, 128], F32)
    pgB = sb("pgB", [128, D], F32)
    gv = sb("gv_sb", [128, 24], F32)
    gt = gv[:, 0:8]
    bng = gv[:, 8:16]
    lng = gv[:, 16:24]
    wa2x = sb("wa2x", [128, 512], BF16)
    mhalf = sb("mhalf", [128, 4], F32)
    ones1 = sb("ones1", [128, 1], F32)
    NX = 3
    xt = [sb(f"xt{i}", [128, D], F32) for i in range(NX)]
    hb = sb("hb", [128, D], BF16)
    hTs = [sb(f"hT{i}", [128, 8, 128], BF16) for i in range(2)]
    uT = sb("uT", [128, 8, 128], F32)
    sgT = sb("sgT", [128, 8, 128], F32)
    vg = sb("vg", [128, D], F32)
    vn = sb("vn", [128, D], BF16)
    sgB = sb("sgB", [128, D], F32)
    vb = sb("vb", [128, D], BF16)
    lrx = sb("lrx", [128, 128], BF16)
    la = sb("la", [128, 512], F32)
    E1 = sb("E1", [128, 512], F32)
    EX = sb("EX", [128, 4, 4], F32)
    EXr = sb("EXr", [128, 4, 4], F32)
    qeT = sb("qeT", [128, 4, 3, 64], BF16)
    keT = sb("keT", [128, 4, 128], BF16)
    kdT = sb("kdT", [128, 4, 128], BF16)
    kd = sb("kd", [128, 512], BF16)
    scTb = sb("scTb", [128, 4, 128], BF16)
    St = sb("St", [128, 4, 256], F32)
    SbA = sb("SbA", [128, 4, 256], BF16)
    SbB = sb("SbB", [128, 4, 256], BF16)
    bo = sb("bo", [128, D], BF16)
    boT = hb[:, :].rearrange("p (c t) -> p c t", c=8)
    aoT = sb("aoT", [128, 8, 128], BF16)
    t2 = sb("t2", [128, D], F32)
    stat = sb("stat", [128, 40], F32)
    smask = sb("smask", [128, 512], BF16)
    bnst = sb("bnst", [128, 12], F32)
    pb = [nc.alloc_psum_tensor(f"pb{i}", [128, 512], F32) for i in range(4)]
    pq = nc.alloc_psum_tensor("pq", [128, 512], F32)
    pk = nc.alloc_psum_tensor("pk", [128, 512], F32)
    po = nc.alloc_psum_tensor("po", [128, 1024], F32)
    rc = [0]

    def rot():
        b = rc[0] % 4
        rc[0] += 1
        return pb[b], f"pb{b}"

    def PE(fn, r=(), w=(), p=()):
        return S.op("pe", fn, reads=r, writes=w, partial=p)

    def ACT(fn, r=(), w=(), p=()):
        return S.op("act", fn, reads=r, writes=w, partial=p)

    def DVE(fn, r=(), w=(), p=()):
        return S.op("dve", fn, reads=r, writes=w, partial=p)

    def POOL(fn, r=(), w=(), p=()):
        return S.op("pool", fn, reads=r, writes=w, partial=p)

    te, ve, se, ge = nc.tensor, nc.vector, nc.scalar, nc.gpsimd

    wnat = t2
    lb1 = sgT
    rsbs = vg
    lb1f = lb1[:, :, :].rearrange("p g t -> p (g t)")
    lns_t = sb("lns_t", [128, 1], F32)
    POOL(lambda: ge.memset(lns_t[:, :], LNS), w=["lns"])
    POOL(lambda: ge.memset(mhalf[:, :], -0.5), w=["mhalf"])
    POOL(lambda: ge.memset(qeT[:, :, :, :].rearrange("p h k t -> p (h k t)"), 0.0), w=["qeT"])
    POOL(lambda: ge.memset(smask[:, :], 1.0), w=["smask"])
    DVE(lambda: ve.memset(smask[:, :].rearrange("p (c t) -> p c t", t=64)[:, :, 0:1], 0.0), p=["smask"])
    POOL(lambda: ge.memset(ones1[:, :], 1.0), w=["ones1"])
    POOL(lambda: ge.memset(lrx[:, :], 1.0), w=["lrx"])
    POOL(lambda: ge.memset(wa2x[:, :], 0.0), w=["wa2x"])
    POOL(lambda: ge.memset(St[:, :, :], 0.0), w=["St"])
    POOL(lambda: ge.memset(lb1f[0:2, :], 1.0), w=["sgT"])
    S.dma("sp", pgB[:, :], post_g.partition_broadcast(128), writes=["pgB"], semkey="pgB")
    S.dma("sp", xt[0][:, :], x[0:128, :], writes=["xt0"], semkey="xt0")
    S.dma("sp", gv[:, :], gvd[:, :], writes=["gt", "bng", "lng"], semkey="gv")
    S.dma("sp", cst[:, :], consts[:, :], writes=["cst"], semkey="cst")
    S.dma("sp", wnat[:, :], w_s[:, :], writes=["t2"], semkey="wnat")
    S.dma("sp", E1[0:17, :], w_a2[:, :], writes=["E1a"], semkey="wa2x")
    S.dma("sp", rsbs[1:2, :], b_s[:, :], writes=["vg"], semkey="bs")
    S.dma("sp", lb1f[0:1, :], ln_b[:, :], partial=["sgT"], semkey="lnb")
    DVE(lambda: ve.tensor_copy(out=wa2x[0:17, :], in_=E1[0:17, :]), r=["E1a"], w=["E1"], p=["wa2x"])
    small = ["cst", "t2", "xt0", "gt", "bng", "lng", "vg", "sgT", "E1a", "pgB"]
    first_w = [True]
    off = 0
    for (k, c0, n) in WBLOCKS:
        S.dma("pool", wi[k][:, :, :].rearrange("p c n -> p (c n)"), w_flat[off * 128:(off + 8 * n) * 128].rearrange("(p m) -> p m", p=128), writes=["wi_" + k], semkey="wi_" + k,
              reads=small if first_w[0] else (), max_dma_last_dim=8192)
        first_w[0] = False
        off += 8 * n
    S.dma("pool", wo[:, 0, :, :].rearrange("p c n -> p (c n)"), wo_flat[0:8192 * 128].rearrange("(p m) -> p m", p=128), writes=["wo0a", "wo0b"], semkey="wo0",
          max_dma_last_dim=8192)
    for q in range(2):
        S.dma("pool", wo[:, 1, q * 8:(q + 1) * 8, :].rearrange("p c n -> p (c n)"),
              wo_flat[(8192 + q * 4096) * 128:(8192 + (q + 1) * 4096) * 128].rearrange("(p m) -> p m", p=128), writes=["wo1" + "ab"[q]], semkey=f"wo1{q}",
              max_dma_last_dim=8192)

    DVE(lambda: ve.tensor_copy(out=identb[:, :], in_=cst[:, K_ID:K_ID + 128]), r=["cst"], w=["identb"])
    pA, tA = rot()
    pB, tB = rot()
    for g in range(8):
        pbk, tk = (pA, tA) if g < 4 else (pB, tB)
        PE(lambda g=g, pbk=pbk: te.transpose(out=pbk[:, (g % 4) * 128:(g % 4 + 1) * 128], in_=wnat[:, g * 128:(g + 1) * 128],
                                             identity=cst[:, K_ID:K_ID + 128]),
           r=["t2", "cst"], w=[tk] if g % 4 == 0 else (), p=() if g % 4 == 0 else [tk])
    wTm = uT
    for g in range(8):
        pbk, tk = (pA, tA) if g < 4 else (pB, tB)
        DVE(lambda g=g, pbk=pbk: ve.tensor_tensor(out=wTm[:, g, :], in0=pbk[:, (g % 4) * 128:(g % 4 + 1) * 128],
                                                  in1=cst[:, K_TRIL:K_TRIL + 128], op=ALU.mult),
            r=[tk, "cst"], p=["uT"])
    DVE(lambda: ve.tensor_copy(out=wTb[:, :, :], in_=wTm[:, :, :]), r=["uT"], w=["wTb"])
    pR0, tR0 = rot()
    pR1, tR1 = rot()
    wTm2 = wTm[:, :, :].rearrange("p g t -> p (g t)")
    PE(lambda: te.matmul(pR0[0:1, :], lhsT=ones1[:, 0:1], rhs=wTm2[:, 0:512], start=True, stop=True), r=["ones1", "uT"], w=[tR0])
    PE(lambda: te.matmul(pR1[0:1, :], lhsT=ones1[:, 0:1], rhs=wTm2[:, 512:1024], start=True, stop=True), r=["ones1", "uT"], w=[tR1])
    DVE(lambda: ve.tensor_copy(out=rsbs[0:1, 0:512], in_=pR0[0:1, :]), r=[tR0], p=["vg"])
    DVE(lambda: ve.tensor_copy(out=rsbs[0:1, 512:1024], in_=pR1[0:1, :]), r=[tR1], p=["vg"])
    pQ0, tQ0 = rot()
    pQ1, tQ1 = rot()
    for g in range(8):
        pbk, tk = (pQ0, tQ0) if g < 4 else (pQ1, tQ1)
        PE(lambda g=g, pbk=pbk: te.matmul(pbk[:, (g % 4) * 128:(g % 4 + 1) * 128], lhsT=lb1f[0:2, g * 128:(g + 1) * 128],
                                          rhs=rsbs[0:2, g * 128:(g + 1) * 128], start=True, stop=True),
           r=["sgT", "vg"], w=[tk] if g % 4 == 0 else (), p=() if g % 4 == 0 else [tk])
    Qf = Qsb[:, :, :].rearrange("p g t -> p (g t)")
    DVE(lambda: ve.tensor_copy(out=Qf[:, 0:512], in_=pQ0[:, :]), r=[tQ0], p=["Qsb"])
    DVE(lambda: ve.tensor_copy(out=Qf[:, 512:1024], in_=pQ1[:, :]), r=[tQ1], p=["Qsb"])


    def load_x(t):
        S.dma("sp", xt[t % NX][:, :], x[t * 128:(t + 1) * 128, :], writes=[f"xt{t % NX}"], semkey=f"xt{t % NX}")

    def stage_P_elem(t):
        xs = xt[t % NX]
        xk = f"xt{t % NX}"
        ACT(lambda: se.activation(out=hb[:, :], in_=xs[:, :], func=AF.Square, accum_out=stat[:, 0:1]),
            r=[xk], w=["hb", "st_ss"])
        DVE(lambda: ve.tensor_scalar(out=stat[:, 1:2], in0=stat[:, 0:1], scalar1=1.0 / D, scalar2=EPS, op0=ALU.mult, op1=ALU.add),
            r=["st_ss"], w=["st_ms"])
        POOL(lambda: ge.tensor_tensor(out=stat[:, 2:3], in0=stat[:, 1:2], in1=mhalf[:, 0:1], op=ALU.pow),
             r=["st_ms", "mhalf"], w=["st_rstd"])
        DVE(lambda: ve.tensor_scalar(out=hb[:, :], in0=xs[:, :], scalar1=stat[:, 2:3], scalar2=None, op0=ALU.mult),
            r=[xk, "st_rstd"], w=["hb"])

    def stage_P_pe(t):
        hT = hTs[t % 2]
        hk = f"hT{t % 2}"
        pt, tk = rot()
        ptb = pt[:, :].bitcast(BF16).rearrange("p (c t) -> p c t", c=8)
        for c in range(8):
            PE(lambda c=c: te.transpose(out=ptb[:, c, :], in_=hb[:, c * 128:(c + 1) * 128], identity=identb[:, :]),
               r=["hb", "identb"], w=[tk] if c == 0 else (), p=() if c == 0 else [tk])
        DVE(lambda: ve.tensor_tensor(out=hT[:, :, :], in0=ptb, in1=gt.unsqueeze(2).to_broadcast([128, 8, 128]), op=ALU.mult),
            r=[tk, "gt"], w=[hk])

    def tok_group(t, col0, wkey):
        hT = hTs[t % 2]
        hk = f"hT{t % 2}"
        pt, tk = rot()
        for c in range(8):
            PE(lambda c=c: te.matmul(pt[:, :], lhsT=hT[:, c, :], rhs=wi[wkey][:, c, :], start=(c == 0), stop=(c == 7)),
               r=[hk, "wi_" + wkey], w=[tk] if c == 0 else (), p=() if c == 0 else [tk])
        return pt, tk

    def feat_group(t, col0, wkey, pt=None, tk=None):
        hT = hTs[t % 2]
        hk = f"hT{t % 2}"
        if pt is None:
            pt, tk = rot()
        first = True
        for j in range(4):
            for c in range(8):
                PE(lambda c=c, j=j: te.matmul(pt[:, j * 128:(j + 1) * 128], lhsT=wi[wkey][:, c, col0 + j * 128:col0 + (j + 1) * 128],
                                              rhs=hT[:, c, :], start=(c == 0), stop=(c == 7)),
                   r=[hk, "wi_" + wkey], w=[tk] if first else (), p=() if first else [tk])
                first = False
        return pt, tk

    def stage_head(t):
        hT = hTs[t % 2]
        hk = f"hT{t % 2}"
        plr, tlr = rot()
        for c in range(8):
            PE(lambda c=c: te.matmul(plr[:, 0:128], lhsT=wi["lqk"][:, c, 0:128], rhs=hT[:, c, :], start=(c == 0), stop=(c == 7)),
               r=[hk, "wi_lqk"], w=[tlr] if c == 0 else (), p=() if c == 0 else [tlr])
        ACT(lambda: se.copy(out=lrx[0:16, :], in_=plr[0:16, 0:128]), r=[tlr], p=["lrx"])
        feat_group(t, 16, "lqk", pq, "pq")
        feat_group(t, 16 + 512, "lqk", pk, "pk")

    def stage_gate(t):
        rot()
        rot()
        pgl, tgl = rot()
        for h in range(4):
            PE(lambda h=h: te.matmul(pgl[:, h * 128:(h + 1) * 128], lhsT=wa2x[:, h * 128:(h + 1) * 128], rhs=lrx[:, :], start=True, stop=True),
               r=["lrx", "wa2x"], w=[tgl] if h == 0 else (), p=() if h == 0 else [tgl])
        ACT(lambda: se.activation(out=la[:, :], in_=pgl[:, :], func=AF.Exp, scale=-1.0), r=[tgl], w=["la"])
        ACT(lambda: se.activation(out=la[:, :], in_=la[:, :], func=AF.Ln, bias=1.0), r=["la"], w=["la"])
        cc = vg[:, 512:1024]
        cc3 = cc.rearrange("p (c t) -> p c t", t=64)
        DVE(lambda: ve.tensor_tensor_scan(out=cc, data0=smask[:, :], data1=la[:, :], initial=0.0, op0=ALU.mult, op1=ALU.add),
            r=["smask", "la"], p=["vg"])
        DVE(lambda: ve.tensor_tensor(out=E1[:, :].rearrange("p (c t) -> p c t", t=64), in0=cc3, in1=cc3[:, :, 31:32].to_broadcast([128, 8, 64]), op=ALU.subtract),
            r=["vg"], w=["E1"])
        DVE(lambda: ve.tensor_tensor(out=la[:, :].rearrange("p (c t) -> p c t", t=64), in0=cc3[:, :, 63:64].to_broadcast([128, 8, 64]), in1=cc3, op=ALU.subtract),
            r=["vg"], w=["la"])
        cc4 = cc.rearrange("p (h c t) -> p h c t", h=4, c=2)
        DVE(lambda: ve.tensor_copy(out=EXr[:, :, 0:2], in_=cc4[:, :, :, 31]), r=["vg"], w=["EXr"])
        DVE(lambda: ve.tensor_copy(out=EXr[:, :, 2:4], in_=cc4[:, :, :, 63]), r=["vg"], p=["EXr"])

    def iteration(t):
        xs = xt[t % NX]
        xk = f"xt{t % NX}"
        if t + 2 < ntiles:
            load_x(t + 2)
        ACT(lambda: se.activation(out=vg[:, 0:512], in_=E1[:, :], func=AF.Exp, scale=1.0 / 16), r=["E1"], p=["vg"])
        ACT(lambda: se.activation(out=E1[:, :], in_=E1[:, :], func=AF.Exp, scale=-1.0 / 16, bias=lns_t[:, 0:1]), r=["E1", "lns"], w=["E1"])
        ACT(lambda: se.activation(out=la[:, :], in_=la[:, :], func=AF.Exp, scale=-1.0 / 16), r=["la"], w=["la"])
        ACT(lambda: se.activation(out=EX[:, :, :].rearrange("p h k -> p (h k)"), in_=EXr[:, :, :].rearrange("p h k -> p (h k)"), func=AF.Exp, scale=-1.0 / 16),
            r=["EXr"], w=["EX"])
        for hh in range(2):
            pt, tk = tok_group(t, C_V + hh * 512, f"v{hh}")
            DVE(lambda pt=pt, hh=hh: ve.tensor_copy(out=vb[:, hh * 512:(hh + 1) * 512], in_=pt[:, :]), r=[tk],
                w=["vb"] if hh == 0 else (), p=() if hh == 0 else ["vb"])
        DVE(lambda: ve.tensor_tensor(out=qeT[:, :, 0:3:2, :], in0=pq[:, :].rearrange("p (h c t) -> p h c t", h=4, c=2),
                                     in1=E1[:, :].rearrange("p (h c t) -> p h c t", h=4, c=2), op=ALU.mult),
            r=["pq", "E1"], p=["qeT"])
        DVE(lambda: ve.tensor_tensor(out=keT[:, :, :].rearrange("p h t -> p (h t)"), in0=pk[:, :], in1=vg[:, 0:512], op=ALU.mult),
            r=["pk", "vg"], w=["keT"])
        DVE(lambda: ve.tensor_tensor(out=kdT[:, :, :].rearrange("p h t -> p (h t)"), in0=pk[:, :], in1=la[:, :], op=ALU.mult),
            r=["pk", "la"], w=["kdT"])
        for h in range(4):
            POOL(lambda h=h: ge.tensor_scalar(out=SbA[:, h, :], in0=St[:, h, :], scalar1=EX[:, h, 0:1], scalar2=1.0, op0=ALU.mult, op1=ALU.mult),
                 r=["St", "EX"], w=["SbA"] if h == 0 else (), p=() if h == 0 else ["SbA"])
        if t + 1 < ntiles:
            stage_P_elem(t + 1)
        for hh in range(2):
            pt, tk = tok_group(t, C_AV + hh * 512, f"av{hh}")
            ACT(lambda pt=pt, hh=hh: se.activation(out=vg[:, hh * 512:(hh + 1) * 512], in_=pt[:, :], func=AF.Gelu), r=[tk],
                w=["vg"] if hh == 0 else (), p=() if hh == 0 else ["vg"])
        for hh in range(2):
            pt, tk = feat_group(t, 0, f"au{hh}")
            ACT(lambda pt=pt, hh=hh: se.activation(out=uT[:, hh * 4:(hh + 1) * 4, :].rearrange("p g t -> p (g t)"), in_=pt[:, :], func=AF.Gelu),
                r=[tk], w=["uT"] if hh == 0 else (), p=() if hh == 0 else ["uT"])
        DVE(lambda: ve.bn_stats(out=bnst[:, 0:6], in_=vg[:, 0:512]), r=["vg"], w=["bnst"])
        DVE(lambda: ve.bn_stats(out=bnst[:, 6:12], in_=vg[:, 512:1024]), r=["vg"], p=["bnst"])
        DVE(lambda: ve.bn_aggr(out=stat[:, 8:10], in_=bnst[:, :]), r=["bnst"], w=["st_mv"])
        DVE(lambda: ve.tensor_scalar(out=stat[:, 10:11], in0=stat[:, 9:10], scalar1=EPS, scalar2=None, op0=ALU.add), r=["st_mv"], w=["st_ve"])
        POOL(lambda: ge.tensor_tensor(out=stat[:, 11:12], in0=stat[:, 10:11], in1=mhalf[:, 0:1], op=ALU.pow), r=["st_ve", "mhalf"], w=["st_lr"])
        pkt, tkt = rot()
        pktb = pkt[:, :].bitcast(BF16)
        for h in range(4):
            PE(lambda h=h: te.transpose(out=pktb[:, h * 128:(h + 1) * 128], in_=kdT[:, h, :], identity=identb[:, :]),
               r=["kdT", "identb"], w=[tkt] if h == 0 else (), p=() if h == 0 else [tkt])
        DVE(lambda: ve.tensor_copy(out=kd[:, :], in_=pktb[:, 0:512]), r=[tkt], w=["kd"])
        psc, tsc = rot()
        for h in range(4):
            PE(lambda h=h: te.matmul(psc[:, h * 128:(h + 1) * 128], lhsT=keT[:, h, :], rhs=qeT[:, h, 0:3:2, :], start=True, stop=True),
               r=["keT", "qeT"], w=[tsc] if h == 0 else (), p=() if h == 0 else [tsc])
        DVE(lambda: ve.tensor_tensor(out=scTb[:, :, :].rearrange("p h t -> p (h t)"), in0=psc[:, :], in1=cst[:, K_MASK:K_MASK + 512], op=ALU.mult),
            r=[tsc, "cst"], w=["scTb"])
        DVE(lambda: ve.scalar_tensor_tensor(out=stat[:, 12:13], in0=stat[:, 8:9], scalar=-1.0, in1=stat[:, 11:12], op0=ALU.mult, op1=ALU.mult),
            r=["st_mv", "st_lr"], w=["st_nmr"])
        ACT(lambda: se.activation(out=vn[:, :], in_=vg[:, :], func=AF.Identity, scale=stat[:, 11:12], bias=stat[:, 12:13]),
            r=["vg", "st_lr", "st_nmr"], w=["vn"])
        for hh in range(2):
            pt, tk = tok_group(t, C_BG + hh * 512, f"bg{hh}")
            ACT(lambda pt=pt, hh=hh: se.activation(out=sgB[:, hh * 512:(hh + 1) * 512], in_=pt[:, :], func=AF.Silu), r=[tk],
                w=["sgB"] if hh == 0 else (), p=() if hh == 0 else ["sgB"])
        for h in range(4):
            hc = slice(h * 256, (h + 1) * 256)
            PE(lambda h=h, hc=hc: te.matmul(po[:, hc], lhsT=scTb[:, h, :], rhs=vb[:, hc], start=(h % 2 == 0), stop=False, skip_group_check=True),
               r=["scTb", "vb"], w=["po"] if h == 0 else (), p=() if h == 0 else ["po"])
        for h in range(4):
            hc = slice(h * 256, (h + 1) * 256)
            PE(lambda h=h, hc=hc: te.matmul(po[:, hc], lhsT=qeT[:, h, 0:2, :].rearrange("p k t -> p (k t)"), rhs=SbA[:, h, :], start=False, stop=False, skip_group_check=True),
               r=["qeT", "SbA"], p=["po"])
        pkvs = [(pq, "pq"), (pk, "pk")]
        for h in range(4):
            pkv, tkv = pkvs[h // 2]
            PE(lambda h=h, pkv=pkv: te.matmul(pkv[:, (h % 2) * 256:(h % 2 + 1) * 256], lhsT=kd[0:64, h * 128:(h + 1) * 128], rhs=vb[0:64, h * 256:(h + 1) * 256],
                                              start=True, stop=True),
               r=["kd", "vb"], w=[tkv] if h % 2 == 0 else (), p=() if h % 2 == 0 else [tkv])
        for h in range(4):
            pkv, tkv = pkvs[h // 2]
            DVE(lambda h=h, pkv=pkv: ve.scalar_tensor_tensor(out=St[:, h, :], in0=St[:, h, :], scalar=EX[:, h, 2:3], in1=pkv[:, (h % 2) * 256:(h % 2 + 1) * 256],
                                                             op0=ALU.mult, op1=ALU.add),
                r=["St", "EX", tkv], p=["St"])
        for h in range(4):
            POOL(lambda h=h: ge.tensor_scalar(out=SbB[:, h, :], in0=St[:, h, :], scalar1=EX[:, h, 1:2], scalar2=1.0, op0=ALU.mult, op1=ALU.mult),
                 r=["St", "EX"], w=["SbB"] if h == 0 else (), p=() if h == 0 else ["SbB"])
        psp = [rot(), rot()]
        for g in range(8):
            pt, tk = psp[g // 4]
            PE(lambda g=g, pt=pt: te.matmul(pt[:, (g % 4) * 128:(g % 4 + 1) * 128], lhsT=vn[:, g * 128:(g + 1) * 128], rhs=wTb[:, g, :], start=True, stop=True),
               r=["vn", "wTb"], w=[tk] if g % 4 == 0 else (), p=() if g % 4 == 0 else [tk])
        t1 = t2[:, :].rearrange("p (g t) -> p g t", g=8)
        for g in range(8):
            pt, tk = psp[g // 4]
            DVE(lambda g=g, pt=pt: ve.scalar_tensor_tensor(out=t1[:, g, :], in0=pt[:, (g % 4) * 128:(g % 4 + 1) * 128], scalar=lng[:, g:g + 1], in1=Qsb[:, g, :],
                                                           op0=ALU.mult, op1=ALU.add),
                r=[tk, "lng", "Qsb"], w=["t2"] if g == 0 else (), p=() if g == 0 else ["t2"])
        POOL(lambda: ge.tensor_tensor(out=uT[:, :, :], in0=uT[:, :, :], in1=t1, op=ALU.mult), r=["uT", "t2"], w=["uT"])
        for hh in range(2):
            pt, tk = feat_group(t, 0, f"ag{hh}")
            ACT(lambda pt=pt, hh=hh: se.activation(out=sgT[:, hh * 4:(hh + 1) * 4, :].rearrange("p g t -> p (g t)"), in_=pt[:, :], func=AF.Silu),
                r=[tk], w=["sgT"] if hh == 0 else (), p=() if hh == 0 else ["sgT"])
        if t + 1 < ntiles:
            stage_P_pe(t + 1)
        for h in range(4):
            hc = slice(h * 256, (h + 1) * 256)
            PE(lambda h=h, hc=hc: te.matmul(po[:, hc], lhsT=qeT[:, h, 1:3, :].rearrange("p k t -> p (k t)"), rhs=SbB[:, h, :], start=False, stop=True, skip_group_check=True),
               r=["qeT", "SbB"], p=["po"])
        pkvs = [rot(), rot()]
        for h in range(4):
            pkv, tkv = pkvs[h // 2]
            PE(lambda h=h, pkv=pkv: te.matmul(pkv[:, (h % 2) * 256:(h % 2 + 1) * 256], lhsT=kd[64:128, h * 128:(h + 1) * 128], rhs=vb[64:128, h * 256:(h + 1) * 256],
                                              start=True, stop=True),
               r=["kd", "vb"], w=[tkv] if h % 2 == 0 else (), p=() if h % 2 == 0 else [tkv])
        for h in range(4):
            ACT(lambda h=h: se.activation(out=bo[:, h * 256:(h + 1) * 256], in_=po[:, h * 256:(h + 1) * 256], func=AF.Square, accum_out=stat[:, 16 + h:17 + h]),
                r=["po"], w=["bo", "st_so"] if h == 0 else (), p=() if h == 0 else ["bo", "st_so"])
        DVE(lambda: ve.tensor_scalar(out=stat[:, 20:24], in0=stat[:, 16:20], scalar1=1.0 / 256, scalar2=EPS, op0=ALU.mult, op1=ALU.add),
            r=["st_so"], w=["st_mo"])
        POOL(lambda: ge.tensor_tensor(out=stat[:, 24:28], in0=stat[:, 20:24], in1=mhalf[:, 0:4], op=ALU.pow), r=["st_mo", "mhalf"], w=["st_ro"])
        for h in range(4):
            DVE(lambda h=h: ve.scalar_tensor_tensor(out=bo[:, h * 256:(h + 1) * 256], in0=po[:, h * 256:(h + 1) * 256], scalar=stat[:, 24 + h:25 + h],
                                                    in1=sgB[:, h * 256:(h + 1) * 256], op0=ALU.mult, op1=ALU.mult),
                r=["po", "st_ro", "sgB"], w=["bo"] if h == 0 else (), p=() if h == 0 else ["bo"])
        for h in range(4):
            pkv, tkv = pkvs[h // 2]
            DVE(lambda h=h, pkv=pkv: ve.scalar_tensor_tensor(out=St[:, h, :], in0=St[:, h, :], scalar=EX[:, h, 3:4], in1=pkv[:, (h % 2) * 256:(h % 2 + 1) * 256],
                                                             op0=ALU.mult, op1=ALU.add),
                r=["St", "EX", tkv], p=["St"])
        POOL(lambda: ge.tensor_tensor(out=aoT[:, :, :], in0=uT[:, :, :], in1=sgT[:, :, :], op=ALU.mult), r=["uT", "sgT"], w=["aoT"])
        if t + 1 < ntiles:
            stage_head(t + 1)
        pbt, tbt = rot()
        pbtb = pbt[:, :].bitcast(BF16).rearrange("p (c t) -> p c t", c=8)
        for c in range(8):
            PE(lambda c=c: te.transpose(out=pbtb[:, c, :], in_=bo[:, c * 128:(c + 1) * 128], identity=identb[:, :]),
               r=["bo", "identb"], w=[tbt] if c == 0 else (), p=() if c == 0 else [tbt])
        DVE(lambda: ve.tensor_tensor(out=boT, in0=pbtb, in1=bng.unsqueeze(2).to_broadcast([128, 8, 128]), op=ALU.mult),
            r=[tbt, "bng"], w=["hb"])
        if t + 1 < ntiles:
            stage_gate(t + 1)
        if t == 0:
            order = [(hh, fc) for hh in range(2) for fc in range(16)]
        else:
            order = [(hh, fc) for half in range(2) for hh in range(2) for fc in range(half * 8, half * 8 + 8)]
        for (hh, fc) in order:
            if True:
                lh = aoT[:, fc, :] if fc < 8 else boT[:, fc - 8, :]
                PE(lambda hh=hh, fc=fc, lh=lh: te.matmul(po[:, hh * 512:(hh + 1) * 512], lhsT=lh, rhs=wo[:, hh, fc, :],
                                                         start=(fc == 0), stop=(fc == 15)),
                   r=["aoT" if fc < 8 else "hb", f"wo{hh}" + ("a" if fc < 8 else "b")], w=["po"] if (hh == 0 and fc == 0) else (), p=() if (hh == 0 and fc == 0) else ["po"])
        ACT(lambda: se.activation(out=t2[:, :], in_=po[:, :], func=AF.Square, accum_out=stat[:, 32:33]), r=["po"], w=["t2", "st_sy"])
        DVE(lambda: ve.tensor_scalar(out=stat[:, 33:34], in0=stat[:, 32:33], scalar1=1.0 / D, scalar2=EPS, op0=ALU.mult, op1=ALU.add),
            r=["st_sy"], w=["st_my"])
        POOL(lambda: ge.tensor_tensor(out=stat[:, 34:35], in0=stat[:, 33:34], in1=mhalf[:, 0:1], op=ALU.pow), r=["st_my", "mhalf"], w=["st_ry"])
        DVE(lambda: ve.scalar_tensor_tensor(out=t2[:, :], in0=po[:, :], scalar=stat[:, 34:35], in1=pgB[:, :], op0=ALU.mult, op1=ALU.mult),
            r=["po", "st_ry", "pgB"], w=["t2"])
        POOL(lambda: ge.tensor_tensor(out=xs[:, :], in0=xs[:, :], in1=t2[:, :], op=ALU.add), r=[xk, "t2"], w=[xk])
        S.dma("sp", out[t * 128:(t + 1) * 128, :], xs[:, :], reads=[xk], semkey="o_" + xk)

    if ntiles > 1:
        load_x(1)
    stage_P_elem(0)
    stage_P_pe(0)
    stage_head(0)
    stage_gate(0)
    for t in range(ntiles):
        iteration(t)
    stats = S.emit()
    return nc, stats


_CACHE = {}


def kernel(x, pre_norm_g, w_in, w_a2, b_a2, a_ln_g, a_ln_b, a_w_s, a_b_s, b_norm_g, w_out, post_norm_g):
    f = lambda a: np.ascontiguousarray(np.asarray(a, dtype=np.float32))
    pc = lambda v: np.ascontiguousarray(v.reshape(8, 128).T)
    x = f(x)
    B = x.shape[0]
    if "nc" not in _CACHE:
        _CACHE["nc"] = build_nc()
    nc, stats = _CACHE["nc"]
    shared = dict(
        gv=np.ascontiguousarray(np.concatenate([pc(f(pre_norm_g)[0]), pc(f(b_norm_g)[0]), pc(f(a_ln_g)[0])], axis=1)), w_in=relayout_w_in(f(w_in)[0]), w_a2=np.ascontiguousarray(np.concatenate([f(w_a2)[0], f(b_a2)[0].reshape(1, 512)], axis=0)),
         ln_b=f(a_ln_b)[0].reshape(1, D), w_s=np.ascontiguousarray(f(a_w_s)[0].transpose(1, 0, 2)).reshape(128, 1024), b_s=f(a_b_s)[0].reshape(1, 1024),
         w_out=relayout_w_out(f(w_out)[0]), post_g=f(post_norm_g)[0].reshape(1, D), consts=host_consts(),
    )
    in_maps = [dict(shared, x=x[i]) for i in range(B)]
    res = run_bass_kernel_spmd(nc, in_maps, core_ids=list(range(B)))
    return np.stack([r["out"] for r in res.results], axis=0).astype(np.float32)
```
